# Optimizing a Trainium2 kernel written in Bass

```python
import math
import functools
import jax
import jax.numpy as jnp
from jax import lax
import numpy as np

D_MODEL = 2048
BATCH = 4
SEQ = 2048
DEPTH = 4
DEC_BATCH = 8
DEC_SEQ = 8
PAST_LEN = 16384
PAGE_SIZE = 128

M_HEADS = 4
M_DK = 256
M_DV = 256
M_WIDTH = M_HEADS * M_DV
QK_WIDTH = 2 * M_HEADS * M_DK
CONV_W = 4
CHUNK = 128
A_HEADS = 8
A_DH = 128
A_WIDTH = A_HEADS * A_DH
PATTERNS = ((128, 1), (512, 4), (2048, 16))
MAX_WINDOW = 2048
Q_BLOCK = 128
N_BUCKETS = 32
MAX_DISTANCE = MAX_WINDOW
D_FF = 5632
EPS = 1e-6
IN_SIZES = (QK_WIDTH, M_WIDTH, M_WIDTH, 2 * M_HEADS, A_WIDTH, A_WIDTH, A_WIDTH, D_MODEL, D_MODEL)
N_IN = QK_WIDTH + 2 * M_WIDTH + 2 * M_HEADS + 3 * A_WIDTH + 2 * D_MODEL

kernel_name = 'dilated_mlstm_macaron_hybrid'


def rmsnorm(x, g):
    x32 = x.astype(jnp.float32)
    y = x32 * lax.rsqrt(jnp.mean(x32 * x32, axis=-1, keepdims=True) + EPS)
    return (y * g.astype(jnp.float32)).astype(x.dtype)


def swiglu(x, w_in, w_out):
    gate, up = jnp.split(x @ w_in, 2, axis=-1)
    return (jax.nn.silu(gate) * up) @ w_out


def split_cols(proj):
    bounds = np.cumsum(IN_SIZES)[:-1].tolist()
    return jnp.split(proj, bounds, axis=-1)


def t5_bucket(dist):
    exact = N_BUCKETS // 2
    d32 = jnp.maximum(dist, 1).astype(jnp.float32)
    large = exact + (jnp.log(d32 / exact) / math.log(MAX_DISTANCE / exact) * (N_BUCKETS - exact)).astype(jnp.int32)
    large = jnp.minimum(large, N_BUCKETS - 1)
    return jnp.where(dist < exact, dist, large)


def pattern_biases(rel_table):
    tab = rel_table.astype(jnp.float32)
    return [tab[t5_bucket(d * jnp.arange(w // d + 1, dtype=jnp.int32))] for (w, d) in PATTERNS]


def combine_groups(parts):
    M = parts[0][2]
    for _, _, m in parts[1:]:
        M = jnp.maximum(M, m)
    wts = [den * jnp.exp(m - M) for _, den, m in parts]
    outs = [num / den[..., None] for num, den, _ in parts]
    total = wts[0][..., None] * outs[0]
    wsum = wts[0]
    for w, o in zip(wts[1:], outs[1:]):
        total = total + w[..., None] * o
        wsum = wsum + w
    return total / wsum[..., None]


def dilated_attention_prompt(q, k, v, biases):
    B, S, H, E = q.shape
    f32 = jnp.float32
    pad = ((0, 0), (MAX_WINDOW, 0), (0, 0), (0, 0))
    kp = jnp.pad(k, pad)
    vp = jnp.pad(v, pad)

    def block(bi):
        b0 = bi * Q_BLOCK
        qb = lax.dynamic_slice_in_dim(q, b0, Q_BLOCK, axis=1)
        parts = []
        for (w, d), bias in zip(PATTERNS, biases):
            nb = w // d
            L = w + Q_BLOCK
            ks = lax.dynamic_slice_in_dim(kp, b0 + MAX_WINDOW - w, L, axis=1)
            vs = lax.dynamic_slice_in_dim(vp, b0 + MAX_WINDOW - w, L, axis=1)
            qr = qb.reshape(B, Q_BLOCK // d, d, H, E)
            kr = ks.reshape(B, L // d, d, H, E)
            vr = vs.reshape(B, L // d, d, H, E)
            s = jnp.einsum('bcrhe,barhe->bhrca', qr, kr).astype(f32)
            c = jnp.arange(Q_BLOCK // d)
            a = jnp.arange(L // d)
            r = jnp.arange(d)
            j = c[:, None] + nb - a[None, :]
            pos = b0 - w + a[None, :] * d + r[:, None]
            valid = ((j >= 0) & (j <= nb))[None] & (pos >= 0)[:, None, :]
            bias_ca = bias[jnp.clip(j, 0, nb)]
            s = s + jnp.transpose(bias_ca, (2, 0, 1))[:, None]
            s = jnp.where(valid, s, -jnp.inf)
            m = jnp.max(s, axis=-1)
            p = jnp.exp(s - m[..., None])
            den = jnp.sum(p, axis=-1)
            num = jnp.einsum('bhrca,barhe->bcrhe', p, vr.astype(f32)).reshape(B, Q_BLOCK, H, E)
            den_q = jnp.transpose(den, (0, 3, 2, 1)).reshape(B, Q_BLOCK, H)
            m_q = jnp.transpose(m, (0, 3, 2, 1)).reshape(B, Q_BLOCK, H)
            parts.append((num, den_q, m_q))
        return combine_groups(parts)

    out = lax.map(block, jnp.arange(S // Q_BLOCK))
    return jnp.transpose(out, (1, 0, 2, 3, 4)).reshape(B, S, H, E)


def dilated_attention_sample(q, k_all, v_all, biases):
    B, T, H, E = q.shape
    f32 = jnp.float32
    qi = k_all.shape[1] - T + jnp.arange(T)
    parts = []
    for (w, d), bias in zip(PATTERNS, biases):
        nb = w // d
        idx = qi[:, None] - d * jnp.arange(nb + 1)[None, :]
        valid = idx >= 0
        idc = jnp.clip(idx, 0)
        kg = k_all[:, idc]
        vg = v_all[:, idc]
        s = jnp.einsum('bthe,btjhe->bhtj', q, kg).astype(f32) + bias.T[None, :, None, :]
        s = jnp.where(valid[None, None], s, -jnp.inf)
        m = jnp.max(s, axis=-1)
        p = jnp.exp(s - m[..., None])
        den = jnp.sum(p, axis=-1)
        num = jnp.einsum('bhtj,btjhe->bthe', p, vg.astype(f32))
        parts.append((num, jnp.transpose(den, (0, 2, 1)), jnp.transpose(m, (0, 2, 1))))
    return combine_groups(parts)


def mlstm_chunk(state, xs):
    C0, n0, m0 = state
    q, k, v, ig, lf = xs
    T = q.shape[1]
    F = jnp.transpose(jnp.cumsum(lf, axis=1), (0, 2, 1))
    igh = jnp.transpose(ig, (0, 2, 1))
    causal = jnp.tril(jnp.ones((T, T), dtype=bool))
    logD = jnp.where(causal, F[..., :, None] - F[..., None, :] + igh[..., None, :], -jnp.inf)
    lstate = F + m0[..., None]
    m = jnp.maximum(lstate, jnp.max(logD, axis=-1))
    wgt = jnp.exp(logD - m[..., None]) * jnp.einsum('bthk,bshk->bhts', q, k)
    sc = jnp.exp(lstate - m)
    num = jnp.einsum('bhts,bshv->bthv', wgt, v) + jnp.transpose(sc, (0, 2, 1))[..., None] * jnp.einsum('bthk,bhkv->bthv', q, C0)
    den = jnp.sum(wgt, axis=-1) + sc * jnp.einsum('bthk,bhk->bht', q, n0)
    h = num / jnp.transpose(jnp.maximum(jnp.abs(den), jnp.exp(-m)), (0, 2, 1))[..., None]
    FT = F[..., -1]
    lT = FT + m0
    logE = FT[..., None] - F + igh
    m_new = jnp.maximum(lT, jnp.max(logE, axis=-1))
    e = jnp.exp(logE - m_new[..., None])
    sT = jnp.exp(lT - m_new)
    C_new = sT[..., None, None] * C0 + jnp.einsum('bhs,bshk,bshv->bhkv', e, k, v)
    n_new = sT[..., None] * n0 + jnp.einsum('bhs,bshk->bhk', e, k)
    return (C_new, n_new, m_new), h


def mlstm(q, k, v, ig, lf, C0, n0, m0):
    B, T = q.shape[0], q.shape[1]
    cl = CHUNK if T % CHUNK == 0 else T
    nc = T // cl

    def to_chunks(t):
        return jnp.swapaxes(t.reshape((B, nc, cl) + t.shape[2:]), 0, 1)

    state, hs = lax.scan(mlstm_chunk, (C0, n0, m0), (to_chunks(q), to_chunks(k), to_chunks(v), to_chunks(ig), to_chunks(lf)))
    return jnp.swapaxes(hs, 0, 1).reshape(B, T, M_HEADS, M_DV), state


def causal_conv(x, buf, w, b):
    T = x.shape[1]
    xx = jnp.concatenate([buf.astype(x.dtype), x], axis=1)
    y = b + xx[:, 0:T] * w[0]
    for i in range(1, CONV_W):
        y = y + xx[:, i:i + T] * w[i]
    return y, xx[:, T:]


def token_mixer(h, w_in, conv_w, conv_b, b_if, m_norm, w_pm, w_pa, w_out, biases, state):
    B, T, _ = h.shape
    f32 = jnp.float32
    dt = h.dtype
    qk_pre, mv, mo, mif, aq, ak, av, gm, ga = split_cols(h @ w_in)
    if state is None:
        conv_buf = jnp.zeros((B, CONV_W - 1, QK_WIDTH), qk_pre.dtype)
        C0 = jnp.zeros((B, M_HEADS, M_DK, M_DV), f32)
        n0 = jnp.zeros((B, M_HEADS, M_DK), f32)
        m0 = jnp.zeros((B, M_HEADS), f32)
    else:
        k_buf, v_buf, C0, n0, m0, conv_buf = state
        C0, n0, m0 = C0.astype(f32), n0.astype(f32), m0.astype(f32)
    qk, conv_new = causal_conv(qk_pre, conv_buf, conv_w, conv_b)
    qk = jax.nn.silu(qk).astype(f32)
    mq = qk[..., :QK_WIDTH // 2].reshape(B, T, M_HEADS, M_DK)
    mk = qk[..., QK_WIDTH // 2:].reshape(B, T, M_HEADS, M_DK) * (M_DK ** -0.5)
    mvv = mv.astype(f32).reshape(B, T, M_HEADS, M_DV)
    gif = mif.astype(f32) + b_if.astype(f32)
    ig = gif[..., :M_HEADS]
    lf = jax.nn.log_sigmoid(gif[..., M_HEADS:])
    hm, (C1, n1, m1) = mlstm(mq, mk, mvv, ig, lf, C0, n0, m0)
    hm = hm * lax.rsqrt(jnp.mean(hm * hm, axis=-1, keepdims=True) + EPS)
    hm = jax.nn.sigmoid(mo.astype(f32)) * hm.reshape(B, T, M_WIDTH) * m_norm.astype(f32)
    aq = aq.reshape(B, T, A_HEADS, A_DH) * (A_DH ** -0.5)
    ak = ak.reshape(B, T, A_HEADS, A_DH)
    av = av.reshape(B, T, A_HEADS, A_DH)
    if state is None:
        ha = dilated_attention_prompt(aq, ak, av, biases)
        keep = min(MAX_WINDOW, T)
        k_new, v_new = ak[:, T - keep:], av[:, T - keep:]
    else:
        k_all = jnp.concatenate([k_buf.astype(ak.dtype), ak], axis=1)
        v_all = jnp.concatenate([v_buf.astype(av.dtype), av], axis=1)
        ha = dilated_attention_sample(aq, k_all, v_all, biases)
        k_new, v_new = ak, av
    merged = jax.nn.sigmoid(gm) * (hm.astype(dt) @ w_pm) + jax.nn.sigmoid(ga) * (ha.reshape(B, T, A_WIDTH).astype(dt) @ w_pa)
    return merged @ w_out, (k_new, v_new, C1, n1, m1, conv_new)


def trunk(x, params, biases, states):
    (w_in, conv_w, conv_b, b_if, m_norm, w_pm, w_pa, w_out, ln_ffa, w_ffa_in, w_ffa_out,
     ln_mix, ln_ffb, w_ffb_in, w_ffb_out, ln_f) = params
    new_states = []
    for l in range(DEPTH):
        x = x + 0.5 * swiglu(rmsnorm(x, ln_ffa[l]), w_ffa_in[l], w_ffa_out[l])
        st = None if states is None else tuple(s[l] for s in states)
        y, st_new = token_mixer(rmsnorm(x, ln_mix[l]), w_in[l], conv_w[l], conv_b[l], b_if[l], m_norm[l],
                                w_pm[l], w_pa[l], w_out[l], biases, st)
        x = x + y
        x = x + 0.5 * swiglu(rmsnorm(x, ln_ffb[l]), w_ffb_in[l], w_ffb_out[l])
        new_states.append(st_new)
    stacked = tuple(jnp.stack(s, axis=0) for s in zip(*new_states))
    return rmsnorm(x, ln_f), stacked


def setup_inputs(seed: int = 0) -> dict:
    key = jax.random.key(seed)
    ks = jax.random.split(key, 32)
    f32 = jnp.float32

    def nrm(k, shape, scale):
        return scale * jax.random.normal(k, shape, f32)

    win = min(MAX_WINDOW, PAST_LEN)
    i_bias = nrm(ks[11], (DEPTH, M_HEADS), 0.1)
    f_bias = jnp.linspace(3.0, 6.0, M_HEADS, dtype=f32)[None, :] + nrm(ks[12], (DEPTH, M_HEADS), 0.1)
    return {
        'x_prompt': nrm(ks[0], (BATCH, SEQ, D_MODEL), 1.0),
        'x_sample': nrm(ks[1], (DEC_BATCH, DEC_SEQ, D_MODEL), 1.0),
        'cache_k': nrm(ks[2], (DEPTH, DEC_BATCH, win, A_HEADS, A_DH), 1.0),
        'cache_v': nrm(ks[3], (DEPTH, DEC_BATCH, win, A_HEADS, A_DH), 1.0),
        'state_C': nrm(ks[4], (DEPTH, DEC_BATCH, M_HEADS, M_DK, M_DV), 0.05),
        'state_n': nrm(ks[5], (DEPTH, DEC_BATCH, M_HEADS, M_DK), 0.05),
        'state_m': nrm(ks[6], (DEPTH, DEC_BATCH, M_HEADS), 0.5),
        'state_conv': nrm(ks[7], (DEPTH, DEC_BATCH, CONV_W - 1, QK_WIDTH), 1.0),
        'w_in': nrm(ks[8], (DEPTH, D_MODEL, N_IN), D_MODEL ** -0.5),
        'conv_w': nrm(ks[9], (DEPTH, CONV_W, QK_WIDTH), CONV_W ** -0.5),
        'conv_b': nrm(ks[10], (DEPTH, QK_WIDTH), 0.01),
        'b_if': jnp.concatenate([i_bias, f_bias], axis=-1),
        'm_norm': 1.0 + nrm(ks[13], (DEPTH, M_WIDTH), 0.01),
        'w_pm': nrm(ks[14], (DEPTH, M_WIDTH, D_MODEL), M_WIDTH ** -0.5),
        'w_pa': nrm(ks[15], (DEPTH, A_WIDTH, D_MODEL), A_WIDTH ** -0.5),
        'w_out': nrm(ks[16], (DEPTH, D_MODEL, D_MODEL), D_MODEL ** -0.5),
        'rel_table': nrm(ks[17], (N_BUCKETS, A_HEADS), 0.5),
        'ln_ffa': 1.0 + nrm(ks[18], (DEPTH, D_MODEL), 0.01),
        'w_ffa_in': nrm(ks[19], (DEPTH, D_MODEL, 2 * D_FF), D_MODEL ** -0.5),
        'w_ffa_out': nrm(ks[20], (DEPTH, D_FF, D_MODEL), D_FF ** -0.5),
        'ln_mix': 1.0 + nrm(ks[21], (DEPTH, D_MODEL), 0.01),
        'ln_ffb': 1.0 + nrm(ks[22], (DEPTH, D_MODEL), 0.01),
        'w_ffb_in': nrm(ks[23], (DEPTH, D_MODEL, 2 * D_FF), D_MODEL ** -0.5),
        'w_ffb_out': nrm(ks[24], (DEPTH, D_FF, D_MODEL), D_FF ** -0.5),
        'ln_f': 1.0 + nrm(ks[25], (D_MODEL,), 0.01),
    }


def reference(x_prompt, x_sample, cache_k, cache_v, state_C, state_n, state_m, state_conv,
              w_in, conv_w, conv_b, b_if, m_norm, w_pm, w_pa, w_out, rel_table,
              ln_ffa, w_ffa_in, w_ffa_out, ln_mix, ln_ffb, w_ffb_in, w_ffb_out, ln_f):
    params = (w_in, conv_w, conv_b, b_if, m_norm, w_pm, w_pa, w_out, ln_ffa, w_ffa_in, w_ffa_out,
              ln_mix, ln_ffb, w_ffb_in, w_ffb_out, ln_f)
    biases = pattern_biases(rel_table)
    y_prompt, (k_p, v_p, C_p, n_p, m_p, conv_p) = trunk(x_prompt, params, biases, None)
    y_sample, (k_s, v_s, C_s, n_s, m_s, conv_s) = trunk(
        x_sample, params, biases, (cache_k, cache_v, state_C, state_n, state_m, state_conv))
    return (y_prompt, y_sample, k_p, v_p, C_p, n_p, m_p, conv_p, k_s, v_s, C_s, n_s, m_s, conv_s)
```

```python
import contextlib
import math
import numpy as np
import concourse.bass as bass
import concourse.mybir as mybir
from concourse.bass_utils import run_bass_kernel_spmd

F32 = mybir.dt.float32
BF16 = mybir.dt.bfloat16
AF = mybir.ActivationFunctionType
ALU = mybir.AluOpType
AX = mybir.AxisListType

D = 2048
DC = 16
NT = 512
NCH = 4
DFF = 5632
FC = 44
NIN = 11272
QK0, MV0, MO0, MIF0, AQ0, AK0, AV0, GM0, GA0 = 0, 2048, 3072, 4096, 4104, 5128, 6152, 7176, 9224
EPS = 1e-6
NEG = -30000.0
LN16 = math.log(16.0)
PATTERNS = ((128, 1), (512, 4), (2048, 16))


class Tl:
    __slots__ = ("t", "w", "r", "name", "v")

    def __init__(self, t, name=""):
        self.t = t
        self.w = None
        self.r = {}
        self.name = name

    def __getitem__(self, k):
        return self.t[k]


class Eng:
    def __init__(self, name, h, sem):
        self.name = name
        self.h = h
        self.sem = sem
        self.count = 0
        self.waited = {}
        self.pend_r = []
        self.pend_w = []
        self.dsems = []
        self.dvals = []
        self.ndma = 0
        self.ngen = 0


class KB:
    def __init__(self, nc, es):
        self.nc = nc
        self.es = es
        self.E = {}
        for name, h in (("pe", nc.tensor), ("act", nc.scalar), ("dve", nc.vector), ("pool", nc.gpsimd), ("sp", nc.sync)):
            sem = es.enter_context(nc.semaphore("s_" + name))
            self.E[name] = Eng(name, h, sem)
        for q, n in (("sp", 8), ("pool", 2)):
            for i in range(n):
                self.E[q].dsems.append(es.enter_context(nc.semaphore(f"d_{q}{i}")))
                self.E[q].dvals.append(0)
            self.E[q].ngen = n

    def sb(self, name, shape, dt):
        self.uid = getattr(self, "uid", 0) + 1
        return Tl(self.es.enter_context(self.nc.sbuf_tensor(f"{name}_{self.uid}", list(shape), dt)), name)

    def ps(self, name, shape, dt):
        return Tl(self.es.enter_context(self.nc.psum_tensor(name, list(shape), dt)), name)

    def _wait(self, E, ev):
        sem, val, src = ev
        k = id(sem)
        if E.waited.get(k, 0) >= val:
            return
        E.h.wait_ge(sem, val)
        E.waited[k] = val

    def _deps(self, E, reads, writes, same_ok):
        for t in reads:
            if t.w is not None and not (same_ok and t.w[2] == E.name):
                self._wait(E, t.w)
        for t in writes:
            if t.w is not None and not (same_ok and t.w[2] == E.name):
                self._wait(E, t.w)
            for ev in t.r.values():
                if ev[2] == E.name:
                    continue
                self._wait(E, ev)

    def op(self, eng, fn, reads=(), writes=(), sig=True):
        E = self.E[eng]
        self._deps(E, reads, writes, same_ok=(eng == "pe"))
        ins = fn(E.h)
        if sig:
            E.count += 1
            ins.then_inc(E.sem, 1)
            ev = (E.sem, E.count, eng)
            for t in E.pend_r:
                t.r[eng] = ev
            for t in E.pend_w:
                t.w = ev
                t.r = {}
            E.pend_r = []
            E.pend_w = []
            for t in reads:
                t.r[eng] = ev
            for t in writes:
                t.w = ev
                t.r = {}
        else:
            for t in reads:
                if not any(t is u for u in E.pend_r):
                    E.pend_r.append(t)
            for t in writes:
                if not any(t is u for u in E.pend_w):
                    E.pend_w.append(t)
        return ins

    def dma(self, q, out, in_, reads=(), writes=(), sem=None, **kw):
        E = self.E[q]
        self._deps(E, reads, writes, same_ok=False)
        if sem is None:
            i = E.ndma % E.ngen
            E.ndma += 1
        else:
            i = sem
        s = E.dsems[i]
        if E.dvals[i] > 0:
            self._wait(E, (s, E.dvals[i], "dma"))
        E.h.dma_start(out=out, in_=in_, **kw).then_inc(s, 16)
        E.dvals[i] += 16
        ev = (s, E.dvals[i], "dma")
        for t in reads:
            t.r[id(s)] = ev
        for t in writes:
            t.w = ev
            t.r = {}
        return ev

    def barrier(self):
        evs = []
        for E in self.E.values():
            if E.count > 0:
                evs.append((E.sem, E.count, E.name))
            for s, v in list(zip(E.dsems, E.dvals))[:E.ngen]:
                if v > 0:
                    evs.append((s, v, "dma"))
        for E in self.E.values():
            if E.name == "pool":
                continue
            for ev in evs:
                if ev[2] == E.name:
                    continue
                self._wait(E, ev)

    def sync_engine(self, name):
        E = self.E[name]
        for Q in self.E.values():
            if Q is E:
                continue
            if Q.count > 0:
                self._wait(E, (Q.sem, Q.count, Q.name))
            for s, v in zip(Q.dsems, Q.dvals):
                if v > 0:
                    self._wait(E, (s, v, "dma"))

    @contextlib.contextmanager
    def scope(self):
        st = contextlib.ExitStack()
        old = self.es
        self.es = st
        try:
            yield
        finally:
            self.barrier()
            st.close()
            self.es = old

    def finish(self):
        E = self.E["sp"]
        for Q in self.E.values():
            for s, v in zip(Q.dsems, Q.dvals):
                if v > 0:
                    self._wait(E, (s, v, "dma"))
        for Q in self.E.values():
            if Q.count > 0 and Q.name != "sp":
                self._wait(E, (Q.sem, Q.count, Q.name))


class WRing:
    def __init__(self, kb, nslot):
        self.kb = kb
        self.slots = [kb.sb(f"wslot{i}", [128, 4096], BF16) for i in range(nslot)]
        E = kb.E["pool"]
        self.base = len(E.dsems)
        for i in range(nslot):
            E.dsems.append(kb.es.enter_context(kb.nc.semaphore(f"d_w{i}")))
            E.dvals.append(0)
        self.n = 0

    def load(self, w2d, k0, nk, c0, ncols):
        i = self.n % len(self.slots)
        self.n += 1
        sl = self.slots[i]
        src = w2d[k0 * 128:(k0 + nk) * 128, c0:c0 + ncols].rearrange("(k p) n -> p k n", p=128)
        sl.v = sl.t[:, 0:nk * ncols].rearrange("p (k n) -> p k n", n=ncols)
        self.kb.dma("pool", sl.v, src, writes=[sl], sem=self.base + i)
        return sl


class PSPool:
    def __init__(self, kb, nf32, nbf):
        self.kb = kb
        self.f = [kb.ps(f"psf{i}", [128, 512], F32) for i in range(nf32)]
        self.b = [kb.ps(f"psb{i}", [128, 1024], BF16) for i in range(nbf)]
        self.ff = list(self.f)
        self.fb = list(self.b)

    def get(self):
        return self.ff.pop(0)

    def put(self, t):
        self.ff.append(t)

    def getb(self):
        return self.fb.pop(0)

    def putb(self, t):
        self.fb.append(t)


NS = 8


class Prog:
    def __init__(self, depth=4, ntile=4, nslot=6, do_mixer=True, mix_parts="ABC", do_prompt=True, do_sample=True, dbg=False, stopA=99):
        self.dbg = dbg
        self.stopA = stopA
        self.depth = depth
        self.ntile = ntile
        self.S = ntile * NT
        self.nslot = nslot
        self.do_mixer = do_mixer
        self.mix_parts = mix_parts
        self.do_prompt = do_prompt
        self.do_sample = do_sample

    def set_mode(self, nt, cl, sample):
        self.NT = nt
        self.CL = cl
        self.NCH = nt // cl
        self.sample = sample

    def build(self):
        nc = bass.Bass("TRN2", target_bir_lowering=False)
        self.nc = nc
        L = self.depth
        S = self.S

        def inp(name, shape, dt=F32):
            return nc.dram_tensor(name, list(shape), dt, kind="ExternalInput").ap()

        def outp(name, shape, dt=F32):
            return nc.dram_tensor(name, list(shape), dt, kind="ExternalOutput").ap()

        def scratch(name, shape, dt):
            return nc.dram_tensor(name, list(shape), dt, kind="ExternalOutput").ap()

        self.w_in = inp("w_in", [L, D, NIN])
        self.w_pm = inp("w_pm", [L, 1024, D])
        self.w_pa = inp("w_pa", [L, 1024, D])
        self.w_out = inp("w_out", [L, D, D])
        self.w_ffa_in = inp("w_ffa_in", [L, D, 2 * DFF])
        self.w_ffa_out = inp("w_ffa_out", [L, DFF, D])
        self.w_ffb_in = inp("w_ffb_in", [L, D, 2 * DFF])
        self.w_ffb_out = inp("w_ffb_out", [L, DFF, D])
        self.ln_all = inp("ln_all", [128, 3 * L + 1, DC])
        self.conv_all = inp("conv_all", [128, L, DC, 5])
        self.bif_all = inp("bif_all", [4, L, 2])
        self.mnorm_all = inp("mnorm_all", [L, 1024])
        self.cst_f32 = inp("cst_f32", [128, 7, 128])
        self.cst_bf = inp("cst_bf", [128, 2, 128], BF16)
        if self.do_prompt:
            self.xT_in = inp("xT", [D, S])
            self.bm_all = inp("bm_all", [128, 8, 5, 128], BF16)
            self.yT_out = outp("yT", [D, S])
            self.k_out = outp("k_p", [L, S, 1024])
            self.v_out = outp("v_p", [L, S, 1024])
            self.C_out = outp("C_p", [L, 4, 2, 128, 257])
            self.m_out = outp("m_p", [L, 4, 1])
            self.conv_out = outp("conv_p", [L, 128, DC, 3])
            self.Vs = scratch("Vs", [L, S, 1024], BF16)
            self.KTs = scratch("KTs", [L, 8, 128, S], BF16)
            self.Cs = scratch("Cs", [L, 128, 4 * 2 * 257], F32)
            self.vs_piece = [[[] for T in range(self.ntile)] for l in range(L)]
            self.kts_piece = [[[] for T in range(self.ntile)] for l in range(L)]
            self.cs_piece = [Tl(None, f"cs{l}") for l in range(L)]
        if self.do_sample:
            self.xsT_in = inp("xsT", [D, NS])
            self.ck_in = inp("ck", [L, 2048, 1024])
            self.cv_in = inp("cv", [L, 2048, 1024])
            self.sC_in = inp("sC", [L, 128, 4 * 2 * 257])
            self.sm_in = inp("sm", [4, L])
            self.sconv_in = inp("sconv", [128, L, DC, 3])
            self.bms_in = inp("bms", [128, 8, 128])
            self.bmn_in = inp("bmn", [NS, 8, NS])
            self.lnc_in = inp("lnc", [128, 2, 128])
            self.ysT_out = outp("ysT", [D, NS])
            self.ks_out = outp("k_s", [L, NS, 1024])
            self.vs_out = outp("v_s", [L, NS, 1024])
            self.Cs_out = outp("C_s", [L, 4, 2, 128, 257])
            self.ms_out = outp("m_s", [L, 4, 1])
            self.convs_out = outp("conv_s", [L, 128, DC, 3])

        if self.dbg:
            self.dbg_hm = outp("dbg_hm", [128, 8, NT], BF16)
            self.dbg_ha = outp("dbg_ha", [128, 8, NT], BF16)
        with contextlib.ExitStack() as es:
            kb = KB(nc, es)
            self.kb = kb
            self.emit()
            kb.finish()
        return nc

    def emit(self):
        kb = self.kb
        L = self.depth
        self.W = WRing(kb, self.nslot)
        self.P = PSPool(kb, 6, 2)
        self.xT = kb.sb("xT_sb", [128, DC, NT], F32)
        self.xn = kb.sb("xn_sb", [128, DC, NT], BF16)
        self.ln_sb = kb.sb("ln_sb", [128, 3 * L + 1, DC], F32)
        self.conv_sb = kb.sb("conv_sb", [128, L, DC, 5], F32)
        self.convh = kb.sb("convh_sb", [128, L, DC, 3], F32)
        self.cf = kb.sb("cf_sb", [128, 7, 128], F32)
        self.cb = kb.sb("cb_sb", [128, 2, 128], BF16)
        self.rstd = kb.sb("rstd_sb", [128, NT], F32)
        self.cc = kb.sb("cc_sb", [128, 4], F32)
        self.bif = kb.sb("bif_sb", [4, L, 2], F32)
        self.nbf = kb.sb("nbf_sb", [4, L], F32)
        self.m0 = kb.sb("m0_sb", [4, L], F32)
        self.wmif = kb.sb("wmif_sb", [128, L, DC, 8], BF16)
        self.mnorm = kb.sb("mnorm_sb", [128, 1024], F32)
        for i, v in enumerate((EPS, 1.0, -LN16, 0.0)):
            kb.op("dve", lambda e: e.memset(self.cc.t[:, i:i + 1], v), writes=[self.cc])
        kb.dma("sp", self.ln_sb.t[:], self.ln_all, writes=[self.ln_sb])
        kb.dma("sp", self.conv_sb.t[:], self.conv_all, writes=[self.conv_sb])
        kb.dma("sp", self.cf.t[:], self.cst_f32, writes=[self.cf])
        kb.dma("sp", self.cb.t[:], self.cst_bf, writes=[self.cb])
        kb.dma("sp", self.bif.t[:], self.bif_all, writes=[self.bif])
        kb.op("dve", lambda e: e.tensor_scalar(out=self.nbf.t[:], in0=self.bif.t[:, :, 1], scalar1=-1.0, scalar2=None,
                                               op0=ALU.mult), reads=[self.bif], writes=[self.nbf])
        for l in range(L):
            kb.dma("pool", self.wmif.t[:, l, :, :],
                   self.w_in[l][:, MIF0:MIF0 + 8].rearrange("(k p) n -> p k n", p=128), writes=[self.wmif])
        self.ident_f = self.cf.t[:, 0, :]
        self.maskup = self.cf.t[:, 1, :]
        self.ones_f = self.cf.t[:, 2, :]
        self.ident_b = self.cb.t[:, 0, :]
        self.ones_b = self.cb.t[:, 1, :]
        self.eps_ap = self.cc.t[:, 0:1]
        self.one_ap = self.cc.t[:, 1:2]
        self.nln16_ap = self.cc.t[:, 2:3]

        if self.do_prompt:
            self.set_mode(NT, 128, False)
            kb.op("dve", lambda e: e.memset(self.convh.t[:], 0.0), writes=[self.convh])
            kb.op("dve", lambda e: e.memset(self.m0.t[:], 0.0), writes=[self.m0])
            for T in range(self.ntile):
                kb.dma("sp", self.xT.t[:], self.xT_in[:, T * NT:(T + 1) * NT].rearrange("(c p) t -> p c t", p=128),
                       writes=[self.xT])
                self.trunk_tile(T, self.yT_out[:, T * NT:(T + 1) * NT])
        if self.do_sample:
            self.set_mode(NS, NS, True)
            kb.dma("sp", self.convh.t[:], self.sconv_in, writes=[self.convh])
            kb.dma("sp", self.m0.t[:], self.sm_in, writes=[self.m0])
            kb.dma("sp", self.xT.t[:, :, 0:NS], self.xsT_in.rearrange("(c p) t -> p c t", p=128), writes=[self.xT])
            with kb.scope():
                self.bms = kb.sb("bms_sb", [128, 8, 128], F32)
                self.bmn = kb.sb("bmn_sb", [NS, 8, NS], F32)
                lnc = kb.sb("lnc_sb", [128, 2, 128], F32)
                kb.dma("sp", self.bms.t[:], self.bms_in, writes=[self.bms])
                kb.dma("sp", self.bmn.t[:], self.bmn_in, writes=[self.bmn])
                kb.dma("sp", lnc.t[:], self.lnc_in, writes=[lnc])
                for h in range(8):
                    kb.op("dve", lambda e: e.tensor_tensor(out=self.bms.t[:, h, :], in0=self.bms.t[:, h, :],
                                                           in1=lnc.t[:, 0, :], op=ALU.add),
                          reads=[self.bms, lnc], writes=[self.bms])
                    kb.op("dve", lambda e: e.tensor_tensor(out=self.bmn.t[:, h, :], in0=self.bmn.t[:, h, :],
                                                           in1=lnc.t[0:NS, 1, 0:NS], op=ALU.add),
                          reads=[self.bmn, lnc], writes=[self.bmn])
                self.trunk_tile(0, self.ysT_out)

    def trunk_tile(self, T, y_dst):
        kb = self.kb
        L = self.depth
        n = self.NT
        for l in range(L):
            self.ffn(l, 0, self.w_ffa_in[l], self.w_ffa_out[l])
            if self.do_mixer:
                self.mixer(T, l)
            self.ffn(l, 2, self.w_ffb_in[l], self.w_ffb_out[l])
        with kb.scope():
            yn = kb.sb("ynorm_sb", [128, DC, n], F32)
            self.rmsnorm(3 * L, dst=yn)
            kb.dma("sp", y_dst.rearrange("(c p) t -> p c t", p=128), yn.t[:], reads=[yn])

    def rmsnorm(self, gi, dst=None):
        kb = self.kb
        n = self.NT
        xT, xn = self.xT, self.xn
        kb.op("act", lambda e: e.activation(out=xn.t[:, :, 0:n], in_=xT.t[:, :, 0:n], func=AF.Square),
              reads=[xT], writes=[xn])
        ps = self.P.get()
        for c in range(DC):
            kb.op("pe", lambda e: e.matmul(ps.t[:, 0:n], self.ones_b, xn.t[:, c, 0:n], start=(c == 0), stop=(c == DC - 1)),
                  reads=[xn, self.cb], writes=[ps], sig=(c == DC - 1))
        rstd = self.rstd
        kb.op("act", lambda e: e.activation(out=rstd.t[:, 0:n], in_=ps.t[:, 0:n], func=AF.Sqrt, bias=self.eps_ap,
                                            scale=1.0 / D), reads=[ps, self.cc], writes=[rstd])
        self.P.put(ps)
        kb.op("dve", lambda e: e.reciprocal(out=rstd.t[:, 0:n], in_=rstd.t[:, 0:n]), reads=[rstd], writes=[rstd])
        if dst is None:
            dst = xn
        for c in range(DC):
            kb.op("dve", lambda e: e.scalar_tensor_tensor(out=dst.t[:, c, 0:n], in0=xT.t[:, c, 0:n],
                                                          scalar=self.ln_sb.t[:, gi, c:c + 1], in1=rstd.t[:, 0:n],
                                                          op0=ALU.mult, op1=ALU.mult),
                  reads=[xT, rstd, self.ln_sb], writes=[dst])

    def proj_fm(self, w2d, c0, ncols, K_chunks, rhs_tile, evac):
        kb, W, P = self.kb, self.W, self.P
        n = self.NT
        ngrp = (ncols + 511) // 512
        segs = [(k0, min(8, K_chunks - k0)) for k0 in range(0, K_chunks, 8)]
        for g in range(ngrp):
            gc = min(512, ncols - g * 512)
            nm = gc // 128
            pts = [P.get() for _ in range(nm)]
            for si, (k0, nk) in enumerate(segs):
                slot = W.load(w2d, k0, nk, c0 + g * 512, gc)
                for m in range(nm):
                    for k in range(nk):
                        first = (si == 0 and k == 0)
                        last = (si == len(segs) - 1 and k == nk - 1)
                        kb.op("pe", lambda e: e.matmul(pts[m].t[:, 0:n], slot.v[:, k, m * 128:(m + 1) * 128],
                                                       rhs_tile.t[:, k0 + k, 0:n], start=first, stop=last),
                              reads=[slot, rhs_tile], writes=[pts[m]], sig=(k == nk - 1))
            for m in range(nm):
                evac(g * 4 + m, pts[m])
                P.put(pts[m])

    def proj_tm(self, w2d, c0, ngrp, evac):
        kb, W, P = self.kb, self.W, self.P
        xn = self.xn
        cl = self.CL
        for g in range(ngrp):
            slots = [W.load(w2d, 0, 8, c0 + g * 512, 512), W.load(w2d, 8, 8, c0 + g * 512, 512)]
            for c in range(self.NCH):
                pt = P.get()
                for k in range(DC):
                    kb.op("pe", lambda e: e.matmul(pt.t[0:cl, :], xn.t[:, k, c * cl:(c + 1) * cl],
                                                   slots[k // 8].v[:, k % 8, :], start=(k == 0), stop=(k == DC - 1)),
                          reads=[slots[k // 8], xn], writes=[pt], sig=(k == DC - 1))
                evac(g, c, pt)
                P.put(pt)

    def ffn(self, l, which, w_in, w_out):
        kb = self.kb
        W, P = self.W, self.P
        n = self.NT
        self.rmsnorm(3 * l + which)
        xn, xT = self.xn, self.xT
        with kb.scope():
            hT = kb.sb("hT_sb", [128, FC, n], BF16)
            sgs = [kb.sb(f"sg{i}", [128, n], F32) for i in range(2)]
            nsg = 0
            for j in range(DFF // 256):
                wg = W.load(w_in, 0, 16, j * 256, 256)
                wu = W.load(w_in, 0, 16, DFF + j * 256, 256)
                for m in range(2):
                    pg = P.get()
                    pu = P.get()
                    for (ws, pt) in ((wg, pg), (wu, pu)):
                        for k in range(DC):
                            kb.op("pe", lambda e: e.matmul(pt.t[:, 0:n], ws.v[:, k, m * 128:(m + 1) * 128],
                                                           xn.t[:, k, 0:n], start=(k == 0), stop=(k == DC - 1)),
                                  reads=[ws, xn], writes=[pt], sig=(k == DC - 1))
                    sg = sgs[nsg % 2]
                    nsg += 1
                    kb.op("act", lambda e: e.activation(out=sg.t[:], in_=pg.t[:, 0:n], func=AF.Silu), reads=[pg], writes=[sg])
                    fc = j * 2 + m
                    kb.op("dve", lambda e: e.tensor_tensor(out=hT.t[:, fc, :], in0=sg.t[:], in1=pu.t[:, 0:n], op=ALU.mult),
                          reads=[sg, pu], writes=[hT])
                    P.put(pg)
                    P.put(pu)

            def evac(c, pt):
                kb.op("dve", lambda e: e.scalar_tensor_tensor(out=xT.t[:, c, 0:n], in0=pt.t[:, 0:n], scalar=0.5,
                                                              in1=xT.t[:, c, 0:n], op0=ALU.mult, op1=ALU.add),
                      reads=[pt, xT], writes=[xT])
            self.proj_fm(w_out, 0, D, FC, hT, evac)

    def mixer(self, T, l):
        kb = self.kb
        n = self.NT
        self.rmsnorm(3 * l + 1)
        with kb.scope():
            self.hmT = kb.sb("hmT_sb", [128, 8, n], BF16)
            if self.sample:
                self.knT = kb.sb("knT_sb", [128, 8, NS], BF16)
                self.vnew = kb.sb("vnew_sb", [NS, 1024], BF16)
            if "A" in self.mix_parts:
                with kb.scope():
                    self.mix_A(T, l)
            else:
                kb.op("dve", lambda e: e.memset(self.hmT.t[:], 0.0), writes=[self.hmT])
            self.haT = kb.sb("haT_sb", [128, 8, n], BF16)
            if "B" in self.mix_parts:
                with kb.scope():
                    if self.sample:
                        self.mix_Bs(l)
                    else:
                        self.mix_B(T, l)
            else:
                kb.op("dve", lambda e: e.memset(self.haT.t[:], 0.0), writes=[self.haT])
            if self.dbg and T == 0 and l == 0:
                kb.dma("sp", self.dbg_hm[:, :, 0:n], self.hmT.t[:], reads=[self.hmT])
                kb.dma("sp", self.dbg_ha[:, :, 0:n], self.haT.t[:], reads=[self.haT])
            with kb.scope():
                self.mix_C(T, l)

    def mix_A(self, T, l):
        kb, P = self.kb, self.P
        xn = self.xn
        n, cl, nch, smp = self.NT, self.CL, self.NCH, self.sample
        w_in = self.w_in[l]
        last_tile = smp or (T == self.ntile - 1)
        k_dst, v_dst = (self.ks_out, self.vs_out) if smp else (self.k_out, self.v_out)
        qkT = kb.sb("qkT_sb", [128, DC, n], BF16)
        cvs = [kb.sb(f"cv{i}", [128, 3 + n], F32) for i in range(2)]
        ycs = [kb.sb(f"yc{i}", [128, n], F32) for i in range(2)]
        vext = kb.sb("vext_sb", [128, nch, 4, 257], BF16)
        og = kb.sb("og_sb", [128, nch, 1024], BF16)
        kst = [kb.sb(f"kst{i}", [128, 512], F32) for i in range(2)]
        tmps = kst
        kbf = [kb.sb(f"kbf{i}", [128, 512], BF16) for i in range(2)]
        KTst = [kb.sb(f"KTst{i}", [128, 4, 128], BF16) for i in range(2)]
        vbf = [kb.sb(f"vbf{i}", [128, 512], BF16) for i in range(2)]
        C = kb.sb("C_sb", [128, 4, 2, 257], F32)
        Cb = kb.sb("Cb_sb", [128, 4, 2, 257], BF16)
        igr = kb.sb("igr_sb", [4, n], F32)
        lp = kb.sb("lp_sb", [4, n], F32)
        Fp = kb.sb("Fp_sb", [4, n], F32)
        arow = kb.sb("arow_sb", [4, n], F32)
        grow = kb.sb("grow_sb", [4, n], F32)
        emn = kb.sb("emn_sb", [4, n], F32)
        onesr = kb.sb("onesr_sb", [4, n], F32)
        cnt = [0]

        Cflat = C.t[:].rearrange("p h j v -> p (h j v)")
        if smp:
            kb.dma("sp", Cflat, self.sC_in[l], writes=[C])
        elif T == 0:
            kb.op("dve", lambda e: e.memset(C.t[:], 0.0), writes=[C])
        else:
            kb.dma("sp", Cflat, self.Cs[l], reads=[self.cs_piece[l]], writes=[C])
        kb.op("act", lambda e: e.activation(out=Cb.t[:], in_=C.t[:], func=AF.Copy), reads=[C], writes=[Cb])
        kb.op("dve", lambda e: e.memset(vext.t[:, :, :, 256:257], 1.0), writes=[vext])
        kb.op("dve", lambda e: e.memset(onesr.t[:], 1.0), writes=[onesr])
        kb.dma("sp", self.mnorm.t[:], self.mnorm_all[l:l + 1, :].partition_broadcast(128), writes=[self.mnorm])

        def evac_qk(ch, pt):
            cv = cvs[cnt[0] % 2]
            yc = ycs[cnt[0] % 2]
            cnt[0] += 1
            cw = self.conv_sb
            kb.op("dve", lambda e: e.tensor_copy(out=cv.t[:, 0:3], in_=self.convh.t[:, l, ch, :]),
                  reads=[self.convh], writes=[cv])
            kb.op("act", lambda e: e.activation(out=cv.t[:, 3:3 + n], in_=pt.t[:, 0:n], func=AF.Copy), reads=[pt], writes=[cv])
            kb.op("dve", lambda e: e.tensor_copy(out=self.convh.t[:, l, ch, :], in_=cv.t[:, n:n + 3]),
                  reads=[cv], writes=[self.convh])
            kb.op("dve", lambda e: e.tensor_scalar(out=yc.t[:], in0=cv.t[:, 0:n], scalar1=cw.t[:, l, ch, 0:1],
                                                   scalar2=cw.t[:, l, ch, 4:5], op0=ALU.mult, op1=ALU.add),
                  reads=[cv, cw], writes=[yc])
            for i in range(1, 4):
                kb.op("dve", lambda e: e.scalar_tensor_tensor(out=yc.t[:], in0=cv.t[:, i:i + n],
                                                              scalar=cw.t[:, l, ch, i:i + 1], in1=yc.t[:],
                                                              op0=ALU.mult, op1=ALU.add),
                      reads=[cv, cw, yc], writes=[yc])
            kb.op("act", lambda e: e.activation(out=qkT.t[:, ch, :], in_=yc.t[:], func=AF.Silu), reads=[yc], writes=[qkT])
        self.proj_fm(w_in, QK0, 2048, DC, xn, evac_qk)
        if last_tile:
            kb.dma("sp", (self.convs_out if smp else self.conv_out)[l], self.convh.t[:, l, :, :], reads=[self.convh])

        if self.stopA == 1:
            kb.op("dve", lambda e: e.memset(self.hmT.t[:], 0.0), writes=[self.hmT])
            return
        pig = P.get()
        pfg = P.get()
        for (pt, c0) in ((pig, 0), (pfg, 4)):
            for k in range(DC):
                kb.op("pe", lambda e: e.matmul(pt.t[0:4, 0:n], self.wmif.t[:, l, k, c0:c0 + 4], xn.t[:, k, 0:n],
                                               start=(k == 0), stop=(k == DC - 1)),
                      reads=[self.wmif, xn], writes=[pt], sig=(k == DC - 1))
        kb.op("dve", lambda e: e.tensor_scalar(out=igr.t[:], in0=pig.t[0:4, 0:n], scalar1=self.bif.t[:, l, 0:1], scalar2=None,
                                               op0=ALU.add), reads=[pig, self.bif], writes=[igr])
        kb.op("act", lambda e: e.activation(out=lp.t[:], in_=pfg.t[0:4, 0:n], func=AF.Exp, bias=self.nbf.t[:, l:l + 1],
                                            scale=-1.0), reads=[pfg, self.nbf], writes=[lp])
        P.put(pig)
        P.put(pfg)
        kb.op("act", lambda e: e.activation(out=lp.t[:], in_=lp.t[:], func=AF.Ln, bias=self.one_ap[0:4, :], scale=1.0),
              reads=[lp, self.cc], writes=[lp])
        kb.op("dve", lambda e: e.tensor_tensor_scan(out=Fp.t[:], data0=onesr.t[:], data1=lp.t[:], initial=0.0,
                                                    op0=ALU.mult, op1=ALU.add), reads=[onesr, lp], writes=[Fp])
        kb.op("dve", lambda e: e.tensor_tensor(out=arow.t[:], in0=igr.t[:], in1=Fp.t[:], op=ALU.add),
              reads=[igr, Fp], writes=[arow])
        kb.op("dve", lambda e: e.tensor_tensor_scan(out=grow.t[:], data0=arow.t[:], data1=arow.t[:],
                                                    initial=self.m0.t[:, l:l + 1], op0=ALU.max, op1=ALU.max),
              reads=[arow, self.m0], writes=[grow])
        kb.op("dve", lambda e: e.tensor_tensor(out=emn.t[:], in0=Fp.t[:], in1=grow.t[:], op=ALU.subtract),
              reads=[Fp, grow], writes=[emn])
        gprev = kb.sb("gprev_sb", [4, nch], F32)
        kb.op("dve", lambda e: e.tensor_copy(out=gprev.t[:, 0:1], in_=self.m0.t[:, l:l + 1]), reads=[self.m0], writes=[gprev])
        for c in range(1, nch):
            kb.op("dve", lambda e: e.tensor_copy(out=gprev.t[:, c:c + 1], in_=grow.t[:, c * cl - 1:c * cl]),
                  reads=[grow], writes=[gprev])
        kb.op("dve", lambda e: e.tensor_scalar(out=self.m0.t[:, l:l + 1], in0=emn.t[:, n - 1:n], scalar1=-1.0,
                                               scalar2=None, op0=ALU.mult), reads=[emn, gprev], writes=[self.m0])
        if last_tile:
            kb.dma("sp", (self.ms_out if smp else self.m_out)[l], self.m0.t[:, l:l + 1], reads=[self.m0])
        kb.op("act", lambda e: e.activation(out=emn.t[:], in_=emn.t[:], func=AF.Exp), reads=[emn], writes=[emn])

        if self.stopA == 2:
            kb.op("dve", lambda e: e.memset(self.hmT.t[:], 0.0), writes=[self.hmT])
            return
        def evac_mv(g, c, pt):
            kb.op("act", lambda e: e.activation(out=vext.t[0:cl, c, 2 * g:2 * g + 2, 0:256],
                                                in_=pt.t[0:cl, :].rearrange("p (h v) -> p h v", h=2), func=AF.Copy),
                  reads=[pt], writes=[vext])
        self.proj_tm(w_in, MV0, 2, evac_mv)

        def evac_mo(g, c, pt):
            tp = tmps[cnt[0] % 2]
            cnt[0] += 1
            kb.op("act", lambda e: e.activation(out=tp.t[0:cl, :], in_=pt.t[0:cl, :], func=AF.Sigmoid), reads=[pt], writes=[tp])
            kb.op("dve", lambda e: e.tensor_tensor(out=og.t[0:cl, c, g * 512:(g + 1) * 512], in0=tp.t[0:cl, :],
                                                   in1=self.mnorm.t[0:cl, g * 512:(g + 1) * 512], op=ALU.mult),
                  reads=[tp, self.mnorm], writes=[og])
        self.proj_tm(w_in, MO0, 2, evac_mo)

        if self.stopA == 3:
            kb.op("dve", lambda e: e.memset(self.hmT.t[:], 0.0), writes=[self.hmT])
            return
        def evac_ak(g, c, pt):
            ks = kst[cnt[0] % 2]
            kbb = kbf[cnt[0] % 2]
            cnt[0] += 1
            kb.op("act", lambda e: e.activation(out=ks.t[0:cl, :], in_=pt.t[0:cl, :], func=AF.Copy), reads=[pt], writes=[ks])
            pos0 = (0 if smp else T * NT) + c * cl
            kb.dma("sp", k_dst[l, pos0:pos0 + cl, g * 512:(g + 1) * 512], ks.t[0:cl, :], reads=[ks])
            kb.op("act", lambda e: e.activation(out=kbb.t[0:cl, :], in_=pt.t[0:cl, :], func=AF.Copy), reads=[pt], writes=[kbb])
            pb = P.getb()
            for i in range(4):
                kb.op("pe", lambda e: e.transpose(pb.t[:, i * cl:(i + 1) * cl], kbb.t[0:cl, i * 128:(i + 1) * 128],
                                                  self.ident_b[0:cl, 0:cl]), reads=[kbb, self.cb], writes=[pb], sig=(i == 3))
            if smp:
                kb.op("act", lambda e: e.activation(out=self.knT.t[:, 4 * g:4 * g + 4, :],
                                                    in_=pb.t[:, 0:4 * cl].rearrange("p (h t) -> p h t", h=4),
                                                    func=AF.Copy), reads=[pb], writes=[self.knT])
                P.putb(pb)
                return
            kt_ = KTst[cnt[0] % 2]
            kb.op("act", lambda e: e.activation(out=kt_.t[:], in_=pb.t[:, 0:512].rearrange("p (h t) -> p h t", h=4),
                                                func=AF.Copy), reads=[pb], writes=[kt_])
            P.putb(pb)
            piece = Tl(None, "ktp")
            kb.dma("sp", self.KTs[l, 4 * g:4 * g + 4, :, pos0:pos0 + 128].rearrange("h p t -> p h t"), kt_.t[:],
                   reads=[kt_], writes=[piece])
            self.kts_piece[l][T].append(piece)
        self.proj_tm(w_in, AK0, 2, evac_ak)

        if self.stopA == 4:
            kb.op("dve", lambda e: e.memset(self.hmT.t[:], 0.0), writes=[self.hmT])
            return
        def evac_av(g, c, pt):
            ks = kst[cnt[0] % 2]
            vb_ = vbf[cnt[0] % 2]
            cnt[0] += 1
            kb.op("act", lambda e: e.activation(out=ks.t[0:cl, :], in_=pt.t[0:cl, :], func=AF.Copy), reads=[pt], writes=[ks])
            pos0 = (0 if smp else T * NT) + c * cl
            kb.dma("sp", v_dst[l, pos0:pos0 + cl, g * 512:(g + 1) * 512], ks.t[0:cl, :], reads=[ks])
            if smp:
                kb.op("act", lambda e: e.activation(out=self.vnew.t[0:cl, g * 512:(g + 1) * 512], in_=pt.t[0:cl, :], func=AF.Copy),
                      reads=[pt], writes=[self.vnew])
                return
            kb.op("act", lambda e: e.activation(out=vb_.t[:], in_=pt.t[:], func=AF.Copy), reads=[pt], writes=[vb_])
            piece = Tl(None, "vsp")
            kb.dma("sp", self.Vs[l, pos0:pos0 + 128, g * 512:(g + 1) * 512], vb_.t[:], reads=[vb_], writes=[piece])
            self.vs_piece[l][T].append(piece)
        self.proj_tm(w_in, AV0, 2, evac_av)

        if self.stopA == 5:
            kb.op("dve", lambda e: e.memset(self.hmT.t[:], 0.0), writes=[self.hmT])
            return
        colsb = kb.sb("colsb_sb", [128, 8], F32)
        GE = kb.sb("GE_sb", [4, 8], F32)
        gebc = kb.sb("gebc_sb", [128, 8], F32)
        ecol = kb.sb("ecol_sb", [128, 4], F32)
        sTc = kb.sb("sTc_sb", [128, 4], F32)
        tDs = [kb.sb(f"tD{i}", [128, 128], F32) for i in range(2)]
        Dts = [kb.sb(f"Dt{i}", [128, 128], F32) for i in range(2)]
        wgs = [kb.sb(f"wg{i}", [128, 128], BF16) for i in range(2)]
        SCs = [kb.sb(f"SC{i}", [128, 128], BF16) for i in range(2)]
        qss = [kb.sb(f"qs{i}", [128, 2, 128], BF16) for i in range(2)]
        hns = [kb.sb(f"hn{i}", [128, 256], F32) for i in range(2)]
        junk = kb.sb("junk_sb", [128, 256], F32)
        sml = [kb.sb(f"sml{i}", [128, 4], F32) for i in range(2)]
        kes = [kb.sb(f"ke{i}", [128, 256], BF16) for i in range(2)]
        hmtok = kb.sb("hmtok_sb", [128, 1024], BF16)
        for c in range(nch):
            cs = slice(c * cl, (c + 1) * cl)
            e_idx = c * cl + cl - 1
            pc = P.get()
            kb.op("pe", lambda e: e.transpose(pc.t[0:cl, 0:4], arow.t[:, cs], self.cf.t[0:4, 0, 0:4]),
                  reads=[arow, self.cf], writes=[pc], sig=False)
            kb.op("pe", lambda e: e.transpose(pc.t[0:cl, 4:8], emn.t[:, cs], self.cf.t[0:4, 0, 0:4]),
                  reads=[emn, self.cf], writes=[pc])
            kb.op("act", lambda e: e.activation(out=colsb.t[0:cl, :], in_=pc.t[0:cl, 0:8], func=AF.Copy), reads=[pc], writes=[colsb])
            P.put(pc)
            kb.op("dve", lambda e: e.tensor_scalar(out=GE.t[:, 0:4], in0=self.cf.t[0:4, 0, 0:4],
                                                   scalar1=grow.t[:, e_idx:e_idx + 1], scalar2=None, op0=ALU.mult),
                  reads=[self.cf, grow], writes=[GE])
            kb.op("dve", lambda e: e.tensor_scalar(out=GE.t[:, 4:8], in0=self.cf.t[0:4, 0, 0:4],
                                                   scalar1=gprev.t[:, c:c + 1], scalar2=None, op0=ALU.mult),
                  reads=[self.cf, gprev], writes=[GE])
            pg = P.get()
            kb.op("pe", lambda e: e.matmul(pg.t[:, 0:8], self.cf.t[0:4, 2, :], GE.t[:], start=True, stop=True),
                  reads=[GE, self.cf], writes=[pg])
            kb.op("act", lambda e: e.activation(out=gebc.t[:], in_=pg.t[:, 0:8], func=AF.Copy), reads=[pg], writes=[gebc])
            P.put(pg)
            kb.op("dve", lambda e: e.tensor_tensor(out=ecol.t[0:cl, :], in0=colsb.t[0:cl, 0:4], in1=gebc.t[0:cl, 0:4],
                                                   op=ALU.subtract), reads=[colsb, gebc], writes=[ecol])
            kb.op("act", lambda e: e.activation(out=ecol.t[0:cl, :], in_=ecol.t[0:cl, :], func=AF.Exp,
                                                bias=self.nln16_ap[0:cl, :]), reads=[ecol, self.cc], writes=[ecol])
            kb.op("dve", lambda e: e.tensor_tensor(out=sTc.t[:], in0=gebc.t[:, 4:8], in1=gebc.t[:, 0:4], op=ALU.subtract),
                  reads=[gebc], writes=[sTc])
            kb.op("act", lambda e: e.activation(out=sTc.t[:], in_=sTc.t[:], func=AF.Exp), reads=[sTc], writes=[sTc])
            pG = P.get()
            for h in range(4):
                kb.op("pe", lambda e: e.matmul(pG.t[:, h * cl:(h + 1) * cl], self.cf.t[0:4, 3 + h, :], grow.t[:, cs],
                                               start=(h == 0), stop=(h == 3), skip_group_check=True),
                      reads=[grow, self.cf], writes=[pG], sig=(h == 3))
            pk = P.getb()
            for i in range(8):
                kb.op("pe", lambda e: e.transpose(pk.t[0:cl, i * 128:(i + 1) * 128], qkT.t[:, 8 + i, cs], self.ident_b),
                      reads=[qkT, self.cb], writes=[pk], sig=(i == 7))
            for h in range(4):
                hs = slice(h * cl, (h + 1) * cl)
                i2 = (c * 4 + h) % 2
                tD, Dt, wg, SC, qs, hn, ke, sm = tDs[i2], Dts[i2], wgs[i2], SCs[i2], qss[i2], hns[i2], kes[i2], sml[i2]
                pS = P.get()
                for j in range(2):
                    kb.op("pe", lambda e: e.matmul(pS.t[0:cl, 0:cl], qkT.t[:, 8 + 2 * h + j, cs], qkT.t[:, 2 * h + j, cs],
                                                   start=(j == 0), stop=(j == 1)), reads=[qkT], writes=[pS], sig=(j == 1))
                kb.op("dve", lambda e: e.scalar_tensor_tensor(out=tD.t[0:cl, 0:cl], in0=pG.t[0:cl, hs], scalar=-1.0,
                                                              in1=self.maskup[0:cl, 0:cl], op0=ALU.mult, op1=ALU.add),
                      reads=[pG, self.cf], writes=[tD])
                kb.op("act", lambda e: e.activation(out=Dt.t[0:cl, 0:cl], in_=tD.t[0:cl, 0:cl], func=AF.Exp,
                                                    bias=colsb.t[0:cl, h:h + 1]), reads=[tD, colsb], writes=[Dt])
                kb.op("dve", lambda e: e.scalar_tensor_tensor(out=wg.t[0:cl, 0:cl], in0=pS.t[0:cl, 0:cl], scalar=0.0625,
                                                              in1=Dt.t[0:cl, 0:cl], op0=ALU.mult, op1=ALU.mult),
                      reads=[pS, Dt], writes=[wg])
                P.put(pS)
                kb.op("act", lambda e: e.activation(out=SC.t[:, 0:cl], in_=pG.t[:, hs], func=AF.Exp,
                                                    bias=gebc.t[:, 4 + h:5 + h], scale=-1.0), reads=[pG, gebc], writes=[SC])
                for j in range(2):
                    kb.op("dve", lambda e: e.tensor_tensor(out=qs.t[:, j, 0:cl], in0=qkT.t[:, 2 * h + j, cs],
                                                           in1=SC.t[:, 0:cl], op=ALU.mult), reads=[qkT, SC], writes=[qs])
                pN = P.get()
                kb.op("pe", lambda e: e.matmul(pN.t[0:cl, 0:257], wg.t[0:cl, 0:cl], vext.t[0:cl, c, h, :],
                                               start=True, stop=False), reads=[wg, vext], writes=[pN], sig=False)
                for j in range(2):
                    kb.op("pe", lambda e: e.matmul(pN.t[0:cl, 0:257], qs.t[:, j, 0:cl], Cb.t[:, h, j, :],
                                                   start=False, stop=(j == 1)), reads=[qs, Cb], writes=[pN], sig=(j == 1))
                kb.op("act", lambda e: e.activation(out=sm.t[0:cl, 0:1], in_=pN.t[0:cl, 256:257], func=AF.Abs),
                      reads=[pN], writes=[sm])
                kb.op("dve", lambda e: e.tensor_scalar(out=sm.t[0:cl, 0:1], in0=sm.t[0:cl, 0:1],
                                                       scalar1=colsb.t[0:cl, 4 + h:5 + h], scalar2=None, op0=ALU.max),
                      reads=[sm, colsb], writes=[sm])
                kb.op("dve", lambda e: e.reciprocal(out=sm.t[0:cl, 1:2], in_=sm.t[0:cl, 0:1]), reads=[sm], writes=[sm])
                kb.op("act", lambda e: e.activation(out=hn.t[0:cl, :], in_=pN.t[0:cl, 0:256], func=AF.Copy,
                                                    scale=sm.t[0:cl, 1:2]), reads=[pN, sm], writes=[hn])
                P.put(pN)
                kb.op("act", lambda e: e.activation(out=junk.t[0:cl, :], in_=hn.t[0:cl, :], func=AF.Square,
                                                    accum_out=sm.t[0:cl, 2:3]), reads=[hn], writes=[junk, sm])
                kb.op("act", lambda e: e.activation(out=sm.t[0:cl, 3:4], in_=sm.t[0:cl, 2:3], func=AF.Sqrt,
                                                    bias=self.eps_ap[0:cl, :], scale=1.0 / 256),
                      reads=[sm, self.cc], writes=[sm])
                kb.op("dve", lambda e: e.reciprocal(out=sm.t[0:cl, 3:4], in_=sm.t[0:cl, 3:4]), reads=[sm], writes=[sm])
                kb.op("dve", lambda e: e.scalar_tensor_tensor(out=hmtok.t[0:cl, h * 256:(h + 1) * 256], in0=hn.t[0:cl, :],
                                                              scalar=sm.t[0:cl, 3:4],
                                                              in1=og.t[0:cl, c, h * 256:(h + 1) * 256],
                                                              op0=ALU.mult, op1=ALU.mult), reads=[hn, sm, og], writes=[hmtok])
                kb.op("dve", lambda e: e.tensor_scalar(out=ke.t[0:cl, :], in0=pk.t[0:cl, h * 256:(h + 1) * 256],
                                                       scalar1=ecol.t[0:cl, h:h + 1], scalar2=None, op0=ALU.mult),
                      reads=[pk, ecol], writes=[ke])
                for j in range(2):
                    pC = P.get()
                    kb.op("pe", lambda e: e.matmul(pC.t[:, 0:257], ke.t[0:cl, j * 128:(j + 1) * 128], vext.t[0:cl, c, h, :],
                                                   start=True, stop=True), reads=[ke, vext], writes=[pC])
                    kb.op("dve", lambda e: e.scalar_tensor_tensor(out=C.t[:, h, j, :], in0=C.t[:, h, j, :],
                                                                  scalar=sTc.t[:, h:h + 1], in1=pC.t[:, 0:257],
                                                                  op0=ALU.mult, op1=ALU.add), reads=[C, sTc, pC], writes=[C])
                    P.put(pC)
                    kb.op("act", lambda e: e.activation(out=Cb.t[:, h, j, :], in_=C.t[:, h, j, :], func=AF.Copy),
                          reads=[C], writes=[Cb])
            P.putb(pk)
            P.put(pG)
            ph = P.getb()
            for i in range(8):
                kb.op("pe", lambda e: e.transpose(ph.t[:, i * cl:(i + 1) * cl], hmtok.t[0:cl, i * 128:(i + 1) * 128],
                                                  self.ident_b[0:cl, 0:cl]), reads=[hmtok, self.cb], writes=[ph], sig=(i == 7))
            kb.op("act", lambda e: e.activation(out=self.hmT.t[:, :, cs],
                                                in_=ph.t[:, 0:8 * cl].rearrange("p (i t) -> p i t", i=8),
                                                func=AF.Copy), reads=[ph], writes=[self.hmT])
            P.putb(ph)
        if last_tile:
            kb.dma("sp", (self.Cs_out if smp else self.C_out)[l].rearrange("h j p v -> p h j v"), C.t[:], reads=[C])
        else:
            kb.dma("sp", self.Cs[l], Cflat, reads=[C], writes=[self.cs_piece[l]])

    def mix_Bs(self, l):
        kb, P = self.kb, self.P
        QT = kb.sb("QTs_sb", [128, 8, NS], BF16)

        def evac_q(ch, pt):
            kb.op("act", lambda e: e.activation(out=QT.t[:, ch, :], in_=pt.t[:, 0:NS], func=AF.Copy, scale=128.0 ** -0.5),
                  reads=[pt], writes=[QT])
        self.proj_fm(self.w_in[l], AQ0, 1024, DC, self.xn, evac_q)
        kb.sync_engine("pool")
        Kc = [kb.sb(f"Kc{i}", [128, 16, 128], BF16) for i in range(2)]
        Vc = [kb.sb(f"Vc{i}", [128, 16, 128], BF16) for i in range(2)]
        KT = [kb.sb(f"KTc{i}", [128, 2048], BF16) for i in range(2)]
        scs = [kb.sb(f"scs{i}", [128, 136], F32) for i in range(2)]
        PT = [kb.sb(f"PTs{i}", [128, 136], BF16) for i in range(2)]
        rds = [kb.sb(f"rds{i}", [128, NS], F32) for i in range(2)]
        for h in range(8):
            b = h % 2
            hc = slice(h * 128, (h + 1) * 128)
            kc, vc, kt, sc, pt, rd = Kc[b], Vc[b], KT[b], scs[b], PT[b], rds[b]
            kb.dma("pool", kc.t[:], self.ck_in[l][:, hc].rearrange("(c p) d -> p c d", p=128), writes=[kc])
            kb.dma("pool", vc.t[:], self.cv_in[l][:, hc].rearrange("(c p) d -> p c d", p=128), writes=[vc])
            for half in range(2):
                pb = P.getb()
                for i in range(8):
                    kb.op("pe", lambda e: e.transpose(pb.t[:, i * 128:(i + 1) * 128], kc.t[:, half * 8 + i, :], self.ident_b),
                          reads=[kc, self.cb], writes=[pb], sig=(i == 7))
                if half == 0:
                    kb.op("act", lambda e: e.activation(out=kt.t[:, 0:1024], in_=pb.t[:], func=AF.Copy),
                          reads=[pb], writes=[kt])
                else:
                    kb.op("act", lambda e: e.activation(out=kt.t[:, 1024:2048], in_=pb.t[:], func=AF.Copy), reads=[pb], writes=[kt])
                P.putb(pb)
            pS = P.get()
            for c in range(16):
                kb.op("pe", lambda e: e.matmul(pS.t[:, c * NS:(c + 1) * NS], kt.t[:, c * 128:(c + 1) * 128], QT.t[:, h, :],
                                               start=True, stop=True, skip_group_check=True),
                      reads=[kt, QT], writes=[pS], sig=False)
            kb.op("pe", lambda e: e.matmul(pS.t[0:NS, 128:128 + NS], self.knT.t[:, h, :], QT.t[:, h, :],
                                           start=True, stop=True, skip_group_check=True),
                  reads=[self.knT, QT], writes=[pS])
            kb.op("dve", lambda e: e.tensor_tensor(out=sc.t[:, 0:128], in0=pS.t[:, 0:128], in1=self.bms.t[:, h, :], op=ALU.add),
                  reads=[pS, self.bms], writes=[sc])
            kb.op("dve", lambda e: e.tensor_tensor(out=sc.t[0:NS, 128:128 + NS], in0=pS.t[0:NS, 128:128 + NS],
                                                   in1=self.bmn.t[:, h, :], op=ALU.add),
                  reads=[pS, self.bmn], writes=[sc])
            P.put(pS)
            kb.op("act", lambda e: e.activation(out=pt.t[:, 0:128], in_=sc.t[:, 0:128], func=AF.Exp), reads=[sc], writes=[pt])
            kb.op("act", lambda e: e.activation(out=pt.t[0:NS, 128:128 + NS], in_=sc.t[0:NS, 128:128 + NS], func=AF.Exp),
                  reads=[sc], writes=[pt])
            pO = P.get()
            pD = P.get()
            for c in range(16):
                kb.op("pe", lambda e: e.matmul(pO.t[:, 0:NS], vc.t[:, c, :], pt.t[:, c * NS:(c + 1) * NS],
                                               start=(c == 0), stop=False), reads=[vc, pt], writes=[pO], sig=False)
                kb.op("pe", lambda e: e.matmul(pD.t[:, 0:NS], self.ones_b, pt.t[:, c * NS:(c + 1) * NS],
                                               start=(c == 0), stop=False), reads=[self.cb, pt], writes=[pD], sig=False)
            kb.op("pe", lambda e: e.matmul(pO.t[:, 0:NS], self.vnew.t[0:NS, hc], pt.t[0:NS, 128:128 + NS],
                                           start=False, stop=True), reads=[self.vnew, pt], writes=[pO])
            kb.op("pe", lambda e: e.matmul(pD.t[:, 0:NS], self.cb.t[0:NS, 1, :], pt.t[0:NS, 128:128 + NS],
                                           start=False, stop=True), reads=[self.cb, pt], writes=[pD])
            kb.op("dve", lambda e: e.reciprocal(out=rd.t[:], in_=pD.t[:, 0:NS]), reads=[pD], writes=[rd])
            kb.op("dve", lambda e: e.tensor_tensor(out=self.haT.t[:, h, :], in0=pO.t[:, 0:NS], in1=rd.t[:], op=ALU.mult),
                  reads=[pO, rd], writes=[self.haT])
            P.put(pO)
            P.put(pD)

    def mix_B(self, T, l):
        kb, P = self.kb, self.P
        nh = T + 1
        Sh = nh * NT
        P0 = T * NT
        KTh = [kb.sb(f"KTh{i}", [128, self.S], BF16) for i in range(2)]
        V1h = [kb.sb(f"V1h{i}", [128, 5, 128], BF16) for i in range(2)]
        V4h = [kb.sb(f"V4h{i}", [128, 2, 4, 128], BF16) for i in range(2)]
        V16h = [kb.sb(f"V16h{i}", [128, 16, 128], BF16) for i in range(2)]
        PTs = [kb.sb(f"PT{i}", [128, NT], BF16) for i in range(6)]
        rden = [kb.sb(f"rden{i}", [128, NT], F32) for i in range(2)]
        npt = [0]
        vsr = [p for t in range(nh) for p in self.vs_piece[l][t]]
        ktr = [p for t in range(nh) for p in self.kts_piece[l][t]]
        self.bm = kb.sb("bm_sb", [128, 8, 5, 128], BF16)
        kb.dma("sp", self.bm.t[:], self.bm_all, writes=[self.bm])
        QT = kb.sb("QT_sb", [128, 8, NT], BF16)

        def evac_q(ch, pt):
            kb.op("act", lambda e: e.activation(out=QT.t[:, ch, :], in_=pt.t[:], func=AF.Copy, scale=128.0 ** -0.5),
                  reads=[pt], writes=[QT])
        self.proj_fm(self.w_in[l], AQ0, 1024, DC, self.xn, evac_q)
        Mk = 32 * nh
        for h in range(8):
            b = h % 2
            kt, v1, v4, v16 = KTh[b], V1h[b], V4h[b], V16h[b]
            hc = slice(h * 128, (h + 1) * 128)
            kb.dma("sp", kt.t[:, 0:Sh], self.KTs[l, h, :, 0:Sh], reads=ktr, writes=[kt])
            if T == 0:
                kb.dma("sp", v1.t[:, 1:5, :], self.Vs[l, 0:NT, hc].rearrange("(c q) d -> q c d", q=128),
                       reads=vsr, writes=[v1])
                kb.dma("sp", v4.t[:, 1, :, :], self.Vs[l, 0:NT, hc].rearrange("(a r) d -> a r d", r=4),
                       reads=vsr, writes=[v4])
            else:
                kb.dma("sp", v1.t[:], self.Vs[l, P0 - 128:P0 + NT, hc].rearrange("(c q) d -> q c d", q=128),
                       reads=vsr, writes=[v1])
                for i in range(2):
                    kb.dma("sp", v4.t[:, i, :, :],
                           self.Vs[l, P0 + (i - 1) * NT:P0 + i * NT, hc].rearrange("(a r) d -> a r d", r=4),
                           reads=vsr, writes=[v4])
            kb.dma("sp", v16.t[0:Mk, :, :], self.Vs[l, 0:Sh, hc].rearrange("(a r) d -> a r d", r=16),
                   reads=vsr, writes=[v16])
            banks = []

            def score_bank(kind):
                pS = P.get()
                first = [True]

                def mm(out, lhsT, rhs, rd):
                    kb.op("pe", lambda e: e.matmul(out, lhsT, rhs, start=first[0], stop=False, skip_group_check=True),
                          reads=rd, writes=[pS], sig=False)
                    first[0] = False
                return pS, mm

            def finish_bank(pS, rows, kind):
                pt = PTs[npt[0] % len(PTs)]
                npt[0] += 1
                kb.op("pe", lambda e: e.matmul(pS.t[0:1, 0:1], self.cb.t[0:1, 0, 0:1], self.cb.t[0:1, 0, 1:2], start=False,
                                               stop=True, skip_group_check=True), reads=[self.cb], writes=[pS])
                kb.op("act", lambda e: e.activation(out=pt.t[0:rows, :], in_=pS.t[0:rows, :], func=AF.Exp),
                      reads=[pS], writes=[pt])
                P.put(pS)
                banks.append((kind, pt))

            pS, mm = score_bank("1s")
            for qb in range(4):
                qs_ = slice(qb * 128, (qb + 1) * 128)
                mm(pS.t[:, qs_], kt.t[:, P0 + qb * 128:P0 + (qb + 1) * 128], QT.t[:, h, qs_], [kt, QT])
                mm(pS.t[:, qs_], self.ident_b, self.bm.t[:, h, 0, :], [self.cb, self.bm])
            finish_bank(pS, 128, "1s")
            pS, mm = score_bank("1p")
            for qb in range(4):
                if T == 0 and qb == 0:
                    continue
                qs_ = slice(qb * 128, (qb + 1) * 128)
                mm(pS.t[:, qs_], kt.t[:, P0 + (qb - 1) * 128:P0 + qb * 128], QT.t[:, h, qs_], [kt, QT])
                mm(pS.t[:, qs_], self.ident_b, self.bm.t[:, h, 1, :], [self.cb, self.bm])
            finish_bank(pS, 128, "1p")
            pS, mm = score_bank("2s")
            for r in range(4):
                qs_ = slice(r * 128, (r + 1) * 128)
                mm(pS.t[:, qs_], kt.t[:, P0 + r:P0 + NT:4], QT.t[:, h, r:NT:4], [kt, QT])
                mm(pS.t[:, qs_], self.ident_b, self.bm.t[:, h, 2, :], [self.cb, self.bm])
            finish_bank(pS, 128, "2s")
            if T > 0:
                pS, mm = score_bank("2p")
                for r in range(4):
                    qs_ = slice(r * 128, (r + 1) * 128)
                    mm(pS.t[:, qs_], kt.t[:, P0 - NT + r:P0:4], QT.t[:, h, r:NT:4], [kt, QT])
                    mm(pS.t[:, qs_], self.ident_b, self.bm.t[:, h, 3, :], [self.cb, self.bm])
                finish_bank(pS, 128, "2p")
            pS, mm = score_bank("3")
            for r in range(16):
                qs_ = slice(r * 32, (r + 1) * 32)
                mm(pS.t[0:Mk, qs_], kt.t[:, r:Sh:16], QT.t[:, h, r:NT:16], [kt, QT])
                mm(pS.t[0:Mk, qs_], self.cb.t[0:Mk, 0, 0:Mk], self.bm.t[0:Mk, h, 4, 32 * T:32 * T + 32], [self.cb, self.bm])
            finish_bank(pS, Mk, "3")
            pO = P.get()
            pD = P.get()
            firstO = [True]

            def pv(outO, outD, lhsV, rhsP, rows, rd):
                kb.op("pe", lambda e: e.matmul(outO, lhsV, rhsP, start=firstO[0], stop=False, skip_group_check=True),
                      reads=rd, writes=[pO], sig=False)
                kb.op("pe", lambda e: e.matmul(outD, self.cb.t[0:rows, 1, :], rhsP, start=firstO[0], stop=False,
                                               skip_group_check=True), reads=rd + [self.cb], writes=[pD], sig=False)
                firstO[0] = False
            for kind, pt in banks:
                if kind == "1s":
                    for qb in range(4):
                        qs_ = slice(qb * 128, (qb + 1) * 128)
                        pv(pO.t[:, qs_], pD.t[:, qs_], v1.t[:, 1 + qb, :], pt.t[:, qs_], 128, [v1, pt])
                elif kind == "1p":
                    for qb in range(4):
                        if T == 0 and qb == 0:
                            continue
                        qs_ = slice(qb * 128, (qb + 1) * 128)
                        pv(pO.t[:, qs_], pD.t[:, qs_], v1.t[:, qb, :], pt.t[:, qs_], 128, [v1, pt])
                elif kind in ("2s", "2p"):
                    i = 1 if kind == "2s" else 0
                    for r in range(4):
                        pv(pO.t[:, r:NT:4], pD.t[:, r:NT:4], v4.t[:, i, r, :], pt.t[:, r * 128:(r + 1) * 128], 128, [v4, pt])
                else:
                    for r in range(16):
                        pv(pO.t[:, r:NT:16], pD.t[:, r:NT:16], v16.t[0:Mk, r, :], pt.t[0:Mk, r * 32:(r + 1) * 32], Mk,
                           [v16, pt])
            kb.op("pe", lambda e: e.matmul(pO.t[0:1, 0:1], self.cb.t[0:1, 0, 1:2], self.cb.t[0:1, 0, 1:2], start=False,
                                           stop=True, skip_group_check=True), reads=[self.cb], writes=[pO, pD])
            rd_ = rden[b]
            kb.op("dve", lambda e: e.reciprocal(out=rd_.t[:], in_=pD.t[:]), reads=[pD], writes=[rd_])
            kb.op("dve", lambda e: e.tensor_tensor(out=self.haT.t[:, h, :], in0=pO.t[:], in1=rd_.t[:], op=ALU.mult),
                  reads=[pO, rd_], writes=[self.haT])
            P.put(pO)
            P.put(pD)

    def mix_C(self, T, l):
        kb, W, P = self.kb, self.W, self.P
        xn, xT = self.xn, self.xT
        n = self.NT
        w_in = self.w_in[l]
        mg = kb.sb("mg_sb", [128, DC, n], BF16)
        sgs = [kb.sb(f"sgm{i}", [128, n], F32) for i in range(4)]
        t1s = [kb.sb(f"t1m{i}", [128, n], F32) for i in range(2)]
        cnt = 0
        for g in range(4):
            gm = [W.load(w_in, 0, 8, GM0 + g * 512, 512), W.load(w_in, 8, 8, GM0 + g * 512, 512)]
            ga = [W.load(w_in, 0, 8, GA0 + g * 512, 512), W.load(w_in, 8, 8, GA0 + g * 512, 512)]
            wpm = W.load(self.w_pm[l], 0, 8, g * 512, 512)
            wpa = W.load(self.w_pa[l], 0, 8, g * 512, 512)
            for m in range(4):
                ms = slice(m * 128, (m + 1) * 128)
                pgm, pga, ppm, ppa = P.get(), P.get(), P.get(), P.get()
                for (ws, pt) in ((gm, pgm), (ga, pga)):
                    for k in range(DC):
                        kb.op("pe", lambda e: e.matmul(pt.t[:, 0:n], ws[k // 8].v[:, k % 8, ms], xn.t[:, k, 0:n],
                                                       start=(k == 0), stop=(k == DC - 1)),
                              reads=[ws[k // 8], xn], writes=[pt], sig=(k == DC - 1))
                for (ws, pt, src) in ((wpm, ppm, self.hmT), (wpa, ppa, self.haT)):
                    for k in range(8):
                        kb.op("pe", lambda e: e.matmul(pt.t[:, 0:n], ws.v[:, k, ms], src.t[:, k, :], start=(k == 0), stop=(k == 7)),
                              reads=[ws, src], writes=[pt], sig=(k == 7))
                s1, s2 = sgs[(2 * cnt) % 4], sgs[(2 * cnt + 1) % 4]
                t1 = t1s[cnt % 2]
                cnt += 1
                kb.op("act", lambda e: e.activation(out=s1.t[:], in_=pgm.t[:, 0:n], func=AF.Sigmoid), reads=[pgm], writes=[s1])
                kb.op("act", lambda e: e.activation(out=s2.t[:], in_=pga.t[:, 0:n], func=AF.Sigmoid), reads=[pga], writes=[s2])
                kb.op("dve", lambda e: e.tensor_tensor(out=t1.t[:], in0=s1.t[:], in1=ppm.t[:, 0:n], op=ALU.mult),
                      reads=[s1, ppm], writes=[t1])
                kb.op("dve", lambda e: e.tensor_tensor(out=s2.t[:], in0=s2.t[:], in1=ppa.t[:, 0:n], op=ALU.mult),
                      reads=[s2, ppa], writes=[s2])
                kb.op("dve", lambda e: e.tensor_tensor(out=mg.t[:, g * 4 + m, :], in0=t1.t[:], in1=s2.t[:], op=ALU.add),
                      reads=[t1, s2], writes=[mg])
                for pt in (pgm, pga, ppm, ppa):
                    P.put(pt)

        def evac(c, pt):
            kb.op("dve", lambda e: e.tensor_tensor(out=xT.t[:, c, 0:n], in0=pt.t[:, 0:n], in1=xT.t[:, c, 0:n], op=ALU.add),
                  reads=[pt, xT], writes=[xT])
        self.proj_fm(self.w_out[l], 0, D, DC, mg, evac)


def t5_bucket_np(dist):
    exact = 16
    d32 = np.maximum(dist, 1).astype(np.float32)
    large = exact + (np.log(d32 / np.float32(exact)) / np.float32(math.log(2048 / exact)) * np.float32(16)).astype(np.int32)
    large = np.minimum(large, 31)
    return np.where(dist < exact, dist, large)


def _consts():
    import ml_dtypes
    cf = np.zeros((128, 7, 128), np.float32)
    cf[:, 0, :] = np.eye(128, dtype=np.float32)
    s = np.arange(128)[:, None]
    t = np.arange(128)[None, :]
    cf[:, 1, :] = np.where(s <= t, 0.0, -1e30).astype(np.float32)
    cf[:, 2, :] = 1.0
    for h in range(4):
        cf[h, 3 + h, :] = 1.0
    cb = np.zeros((128, 2, 128), ml_dtypes.bfloat16)
    cb[:, 0, :] = np.eye(128)
    cb[:, 1, :] = 1.0
    return cf, cb


def _bm_tiles(rel_table):
    import ml_dtypes
    s = np.arange(128)[:, None]
    t = np.arange(128)[None, :]
    out = np.zeros((128, 8, 5, 128), np.float32)
    kinds = [(1, False), (1, True), (4, False), (4, True), (16, False)]
    for i, (d, prev) in enumerate(kinds):
        j = t - s + (128 if prev else 0)
        valid = (j >= 0) & (j <= 128)
        idx = t5_bucket_np(np.clip(j, 0, 128) * d)
        vals = rel_table[idx]
        out[:, :, i, :] = np.where(valid[:, None, :], np.transpose(vals, (0, 2, 1)), np.float32(NEG))
    return out.astype(ml_dtypes.bfloat16)


def _prep_shared(p, L):
    ln = np.concatenate([np.stack([p["ln_ffa"][l], p["ln_mix"][l], p["ln_ffb"][l]]) for l in range(L)]
                        + [p["ln_f"][None]], axis=0)
    ln_all = np.ascontiguousarray(ln.reshape(3 * L + 1, DC, 128).transpose(2, 0, 1))
    cw = np.concatenate([p["conv_w"][:L], p["conv_b"][:L, None, :]], axis=1)
    conv_all = np.ascontiguousarray(cw.reshape(L, 5, DC, 128).transpose(3, 0, 2, 1))
    bif_all = np.ascontiguousarray(p["b_if"][:L].reshape(L, 2, 4).transpose(2, 0, 1))
    cf, cb = _consts()
    sh = {
        "w_in": p["w_in"][:L], "w_pm": p["w_pm"][:L], "w_pa": p["w_pa"][:L], "w_out": p["w_out"][:L],
        "w_ffa_in": p["w_ffa_in"][:L], "w_ffa_out": p["w_ffa_out"][:L],
        "w_ffb_in": p["w_ffb_in"][:L], "w_ffb_out": p["w_ffb_out"][:L],
        "ln_all": ln_all.astype(np.float32), "conv_all": conv_all.astype(np.float32),
        "bif_all": bif_all.astype(np.float32), "mnorm_all": np.ascontiguousarray(p["m_norm"][:L]),
        "bm_all": _bm_tiles(np.asarray(p["rel_table"], np.float32)),
        "cst_f32": cf, "cst_bf": cb,
    }
    return sh


def _count_np(delta):
    c = ((delta >= 0) & (delta <= 128)).astype(np.int32)
    c = c + ((delta >= 0) & (delta % 4 == 0) & (delta <= 512))
    c = c + ((delta >= 0) & (delta % 16 == 0) & (delta <= 2048))
    return c


def _sample_tiles(rel_table):
    rel_table = np.asarray(rel_table, np.float32)
    p = np.arange(128)[:, None]
    col = np.arange(128)[None, :]
    c, q = col // NS, col % NS
    delta = 2048 + q - (c * 128 + p)
    cnt = _count_np(delta)
    vals = rel_table[t5_bucket_np(np.clip(delta, 0, 2048))]
    bms = np.where((cnt > 0)[:, None, :], np.transpose(vals, (0, 2, 1)), np.float32(0.0)).astype(np.float32)
    lnc = np.zeros((128, 2, 128), np.float32)
    lnc[:, 0, :] = np.where(cnt > 0, np.log(np.maximum(cnt, 1)), NEG)
    kk = np.arange(NS)[:, None]
    qq = np.arange(NS)[None, :]
    dn = qq - kk
    cn = _count_np(dn)
    valn = rel_table[t5_bucket_np(np.clip(dn, 0, 2048))]
    bmn = np.where((cn > 0)[:, None, :], np.transpose(valn, (0, 2, 1)), np.float32(0.0)).astype(np.float32)
    lnc[0:NS, 1, 0:NS] = np.where(cn > 0, np.log(np.maximum(cn, 1)), NEG)
    return {"bms": np.ascontiguousarray(bms), "bmn": np.ascontiguousarray(bmn), "lnc": lnc}


def _sample_state(p, L, b):
    sC = np.concatenate([p["state_C"][:L, b], p["state_n"][:L, b][..., None]], axis=-1)
    sC = sC.reshape(L, 4, 2, 128, 257).transpose(0, 3, 1, 2, 4).reshape(L, 128, 4 * 2 * 257)
    return {
        "xsT": np.ascontiguousarray(p["x_sample"][b].T),
        "ck": np.ascontiguousarray(p["cache_k"][:L, b].reshape(L, 2048, 1024)),
        "cv": np.ascontiguousarray(p["cache_v"][:L, b].reshape(L, 2048, 1024)),
        "sC": np.ascontiguousarray(sC),
        "sm": np.ascontiguousarray(p["state_m"][:L, b].T),
        "sconv": np.ascontiguousarray(p["state_conv"][:L, b].reshape(L, 3, DC, 128).transpose(3, 0, 2, 1)),
    }


def _unpack_states(r, sfx, L):
    Cg = r["C_" + sfx]
    C = Cg[..., :256].reshape(L, 4, 256, 256)
    n = Cg[..., 256].reshape(L, 4, 256)
    m = r["m_" + sfx].reshape(L, 4)
    conv = r["conv_" + sfx].transpose(0, 3, 2, 1).reshape(L, 3, 2048)
    k = r["k_" + sfx]
    v = r["v_" + sfx]
    return k.reshape(L, k.shape[1], 8, 128), v.reshape(L, v.shape[1], 8, 128), C, n, m, conv


def kernel(**inputs):
    L = 4
    p = {k: np.asarray(v) for k, v in inputs.items()}
    sh = _prep_shared(p, L)
    sh.update(_sample_tiles(p["rel_table"]))
    in_maps = []
    for core in range(8):
        m = dict(sh)
        m["xT"] = np.ascontiguousarray(p["x_prompt"][core % 4].T)
        m.update(_sample_state(p, L, core))
        in_maps.append(m)
    nc = Prog(depth=L, ntile=4).build()
    res = run_bass_kernel_spmd(nc, in_maps, core_ids=list(range(8)))
    rs = res.results
    f32 = np.float32
    y_p = np.stack([rs[b]["yT"].T for b in range(4)]).astype(f32)
    y_s = np.stack([rs[b]["ysT"].T for b in range(8)]).astype(f32)
    P_ = [_unpack_states(rs[b], "p", L) for b in range(4)]
    S_ = [_unpack_states(rs[b], "s", L) for b in range(8)]
    outs = [y_p, y_s]
    for grp in (P_, S_):
        for i in range(6):
            outs.append(np.ascontiguousarray(np.stack([g[i] for g in grp], axis=1)).astype(f32))
    return tuple(outs)
```
